# Optimizing a Trainium2 kernel written in Bass

```python
import jax, jax.numpy as jnp
from jax import lax
import numpy as np

D_MODEL = 1024
BATCH = 16
SEQ = 2048
DEPTH = 4

D_CONV = D_MODEL
CONV_A_WIDTH = 31
D_RNN = 1536
N_RNN_HEADS = 16
RNN_HEAD_DIM = D_RNN // N_RNN_HEADS
CONV_B_WIDTH = 4
LRU_C = 8.0
D_FF = 4 * D_MODEL
EPS = 1e-6

SPLITS = (D_CONV, D_CONV, D_RNN, D_RNN, D_MODEL, D_MODEL)
D_IN = sum(SPLITS)

kernel_name = "hybrid_conformer_conv_rglru_gated_parallel"


def rms_norm(x, g):
    xf = x.astype(jnp.float32)
    y = xf * lax.rsqrt(jnp.mean(xf * xf, axis=-1, keepdims=True) + EPS)
    return (y * g.astype(jnp.float32)).astype(x.dtype)


def layer_norm(x, g, b):
    xf = x.astype(jnp.float32)
    mu = jnp.mean(xf, axis=-1, keepdims=True)
    xc = xf - mu
    y = xc * lax.rsqrt(jnp.mean(xc * xc, axis=-1, keepdims=True) + EPS)
    return (y * g.astype(jnp.float32) + b.astype(jnp.float32)).astype(x.dtype)


def causal_depthwise_conv(u, w, b):
    k = w.shape[0]
    y = lax.conv_general_dilated(
        u, w[:, None, :].astype(u.dtype), window_strides=(1,), padding=[(k - 1, 0)],
        dimension_numbers=("NWC", "WIO", "NWC"), feature_group_count=u.shape[-1])
    return y + b


def block_diag_linear(x, w, b):
    bsz, s, c = x.shape
    xh = x.reshape(bsz, s, N_RNN_HEADS, RNN_HEAD_DIM)
    y = jnp.einsum("bshi,hij->bshj", xh, w).reshape(bsz, s, c)
    return y + b


def rg_lru(x, w_a, b_a, w_x, b_x, lam):
    s = x.shape[1]
    r = jax.nn.sigmoid(block_diag_linear(x, w_a, b_a).astype(jnp.float32))
    i = jax.nn.sigmoid(block_diag_linear(x, w_x, b_x).astype(jnp.float32))
    log_a = -LRU_C * r * jax.nn.softplus(-lam.astype(jnp.float32))
    a = jnp.exp(log_a)
    mult = jnp.sqrt(-jnp.expm1(2.0 * log_a))
    is_start = (jnp.arange(s) == 0)[None, :, None]
    mult = jnp.where(is_start, 1.0, mult)
    bterm = mult * i * x.astype(jnp.float32)

    def combine(left, right):
        a_l, b_l = left
        a_r, b_r = right
        return a_r * a_l, a_r * b_l + b_r

    _, h = lax.associative_scan(combine, (a, bterm), axis=1)
    return h.astype(x.dtype)


def setup_inputs(seed: int = 0) -> dict:
    key = jax.random.key(seed)
    ks = jax.random.split(key, 32)
    f32 = jnp.float32
    L = DEPTH

    def nrm(k, shape, scale):
        return jax.random.normal(k, shape, f32) * scale

    x = jax.random.normal(ks[0], (BATCH, SEQ, D_MODEL), f32)
    g_mix = 1.0 + nrm(ks[1], (L, D_MODEL), 0.02)
    w_in = nrm(ks[2], (L, D_MODEL, D_IN), D_MODEL ** -0.5)
    b_in = nrm(ks[3], (L, D_IN), 0.02)
    conv_a_w = nrm(ks[4], (L, CONV_A_WIDTH, D_CONV), CONV_A_WIDTH ** -0.5)
    conv_a_b = nrm(ks[5], (L, D_CONV), 0.02)
    ln_g = 1.0 + nrm(ks[6], (L, D_CONV), 0.02)
    ln_b = nrm(ks[7], (L, D_CONV), 0.02)
    w_a_out = nrm(ks[8], (L, D_CONV, D_MODEL), D_CONV ** -0.5)
    conv_b_w = nrm(ks[9], (L, CONV_B_WIDTH, D_RNN), CONV_B_WIDTH ** -0.5)
    conv_b_b = nrm(ks[10], (L, D_RNN), 0.02)
    w_rg_a = nrm(ks[11], (L, N_RNN_HEADS, RNN_HEAD_DIM, RNN_HEAD_DIM), RNN_HEAD_DIM ** -0.5)
    b_rg_a = nrm(ks[12], (L, D_RNN), 0.02)
    w_rg_x = nrm(ks[13], (L, N_RNN_HEADS, RNN_HEAD_DIM, RNN_HEAD_DIM), RNN_HEAD_DIM ** -0.5)
    b_rg_x = nrm(ks[14], (L, D_RNN), 0.02)
    a0 = jax.random.uniform(ks[15], (L, D_RNN), f32, 0.9, 0.999)
    s0 = a0 ** (1.0 / LRU_C)
    lam = jnp.log(s0) - jnp.log1p(-s0)
    w_b_out = nrm(ks[16], (L, D_RNN, D_MODEL), D_RNN ** -0.5)
    w_o = nrm(ks[17], (L, D_MODEL, D_MODEL), D_MODEL ** -0.5)
    g_mlp = 1.0 + nrm(ks[18], (L, D_MODEL), 0.02)
    w_1 = nrm(ks[19], (L, D_MODEL, D_FF), D_MODEL ** -0.5)
    w_2 = nrm(ks[20], (L, D_FF, D_MODEL), D_FF ** -0.5)
    g_final = 1.0 + nrm(ks[21], (D_MODEL,), 0.02)
    return {"x": x, "g_mix": g_mix, "w_in": w_in, "b_in": b_in,
            "conv_a_w": conv_a_w, "conv_a_b": conv_a_b, "ln_g": ln_g, "ln_b": ln_b,
            "w_a_out": w_a_out, "conv_b_w": conv_b_w, "conv_b_b": conv_b_b,
            "w_rg_a": w_rg_a, "b_rg_a": b_rg_a, "w_rg_x": w_rg_x, "b_rg_x": b_rg_x,
            "lam": lam, "w_b_out": w_b_out, "w_o": w_o, "g_mlp": g_mlp,
            "w_1": w_1, "w_2": w_2, "g_final": g_final}


def reference(x, g_mix, w_in, b_in, conv_a_w, conv_a_b, ln_g, ln_b, w_a_out,
              conv_b_w, conv_b_b, w_rg_a, b_rg_a, w_rg_x, b_rg_x, lam, w_b_out,
              w_o, g_mlp, w_1, w_2, g_final):
    cuts = np.cumsum(SPLITS)[:-1].tolist()
    for l in range(DEPTH):
        h = rms_norm(x, g_mix[l])
        z = jnp.einsum("bsd,de->bse", h, w_in[l]) + b_in[l]
        va, ga, xb, gb, sa, sb = jnp.split(z, cuts, axis=-1)

        u = va * jax.nn.sigmoid(ga)
        u = causal_depthwise_conv(u, conv_a_w[l], conv_a_b[l])
        u = jax.nn.silu(layer_norm(u, ln_g[l], ln_b[l]))
        y_a = jnp.einsum("bsc,cd->bsd", u, w_a_out[l])

        v = causal_depthwise_conv(xb, conv_b_w[l], conv_b_b[l])
        v = rg_lru(v, w_rg_a[l], b_rg_a[l], w_rg_x[l], b_rg_x[l], lam[l])
        y_b = jnp.einsum("bsc,cd->bsd", v * jax.nn.gelu(gb), w_b_out[l])

        m = jax.nn.sigmoid(sa) * y_a + jax.nn.sigmoid(sb) * y_b
        x = x + jnp.einsum("bsd,de->bse", m, w_o[l])

        h = rms_norm(x, g_mlp[l])
        f = jnp.square(jax.nn.relu(jnp.einsum("bsd,df->bsf", h, w_1[l])))
        x = x + jnp.einsum("bsf,fd->bsd", f, w_2[l])
    return rms_norm(x, g_final)
```

```python
import numpy as np
from contextlib import ExitStack

import concourse.bass as bass
import concourse.mybir as mybir
from concourse.bass_utils import run_bass_kernel_spmd

F32 = mybir.dt.float32
BF16 = mybir.dt.bfloat16
AF = mybir.ActivationFunctionType
ALU = mybir.AluOpType

D = 1024
NC8 = 8
D_RNN = 1536
NR = 12
NF = 32
KA = 31
KB = 4
T = 512
EPS = 1e-6
WT_PER_UNIT = 32
UNIT = WT_PER_UNIT * 128
BD_DEPS = {0: (0, 1), 1: (0, 1, 2), 2: (1, 2)}
KP = 16
N_WT = 1240 + 4 * KP + 4 * KA
GELU_K = 1.5957691216057308
GELU_C = 0.044715

PO = {}
_o = 0
for _n, _w in (("b_in", 56), ("caw", 8 * KA), ("cab", 8), ("lng", 8), ("lnb", 8),
               ("cbw", NR * KB), ("cbb", NR), ("bra", NR), ("brx", NR), ("lam", NR),
               ("gmix", 8), ("gmlp", 8), ("gfin", 8), ("clam", NR), ("clam2", NR),
               ("tmp", NR), ("hb_in", 56), ("hbra", NR), ("hbrx", NR), ("hclam", NR)):
    PO[_n] = _o
    _o += _w
NP = _o


def _layer_wtiles(l, w_in, w_a_out, w_b_out, w_o, w_1, w_2, w_rg_a, w_rg_x, conv_a_w):
    tiles = []
    win = w_in[l]

    def col(w, c0, nk):
        for k in range(nk):
            tiles.append(w[k * 128:(k + 1) * 128, c0:c0 + 128])

    bd = []
    for w in (w_rg_a[l], w_rg_x[l]):
        m = np.zeros((D_RNN, D_RNN), np.float32)
        for h in range(16):
            m[h * 96:(h + 1) * 96, h * 96:(h + 1) * 96] = w[h]
        bd.append(m)
    def conv_tiles(c):
        for k in range(KP if c < 4 else KA):
            tiles.append(np.diag(conv_a_w[l][k, c * 128:(c + 1) * 128]))

    for c in range(8):
        col(win, 0 + c * 128, 8)
        col(win, 1024 + c * 128, 8)
        if 1 <= c <= 4:
            conv_tiles(c - 1)
    for g in range(4):
        for j in range(3):
            col(win, 2048 + (3 * g + j) * 128, 8)
        for j in range(3):
            for m in bd:
                for ci in BD_DEPS[j]:
                    tiles.append(m[(3 * g + ci) * 128:(3 * g + ci + 1) * 128,
                                   (3 * g + j) * 128:(3 * g + j + 1) * 128])
            col(win, 3584 + (3 * g + j) * 128, 8)
        if g < 2:
            conv_tiles(4 + 2 * g)
            conv_tiles(5 + 2 * g)
    def sasb(co):
        col(win, 5120 + co * 128, 8)
        col(win, 6144 + co * 128, 8)

    sasb(0)
    sasb(1)
    for co in range(8):
        col(w_a_out[l], co * 128, 8)
        col(w_b_out[l], co * 128, 12)
        if co + 2 < 8:
            sasb(co + 2)
    for co in range(8):
        col(w_o[l], co * 128, 8)
    for fc in range(NF):
        col(w_1[l], fc * 128, 8)
    for co in range(8):
        col(w_2[l], co * 128, 32)
    assert len(tiles) == N_WT
    return tiles


def _pack_weights(L, **w):
    nu = (N_WT + WT_PER_UNIT - 1) // WT_PER_UNIT
    out = np.zeros((L, nu, 128, UNIT), np.float32)
    for l in range(L):
        tiles = _layer_wtiles(l, **w)
        for i, t in enumerate(tiles):
            u, p = divmod(i, WT_PER_UNIT)
            out[l, u, :, p * 128:(p + 1) * 128] = t
    return out


def _pack_params(L, b_in, conv_a_w, conv_a_b, ln_g, ln_b, conv_b_w, conv_b_b,
                 b_rg_a, b_rg_x, lam, g_mix, g_mlp, g_final):
    P = np.zeros((L, 128, NP), np.float32)

    def fm(v, n):
        return np.ascontiguousarray(v.reshape(n, 128).T)

    for l in range(L):
        P[l, :, PO["b_in"]:PO["b_in"] + 56] = fm(b_in[l], 56)
        P[l, :, PO["caw"]:PO["caw"] + 8 * KA] = \
            conv_a_w[l].reshape(KA, 8, 128).transpose(2, 1, 0).reshape(128, 8 * KA)
        P[l, :, PO["cab"]:PO["cab"] + 8] = fm(conv_a_b[l], 8)
        P[l, :, PO["lng"]:PO["lng"] + 8] = fm(ln_g[l], 8)
        P[l, :, PO["lnb"]:PO["lnb"] + 8] = fm(ln_b[l], 8)
        P[l, :, PO["cbw"]:PO["cbw"] + NR * KB] = \
            conv_b_w[l].reshape(KB, NR, 128).transpose(2, 1, 0).reshape(128, NR * KB)
        P[l, :, PO["cbb"]:PO["cbb"] + NR] = fm(conv_b_b[l], NR)
        P[l, :, PO["bra"]:PO["bra"] + NR] = fm(b_rg_a[l], NR)
        P[l, :, PO["brx"]:PO["brx"] + NR] = fm(b_rg_x[l], NR)
        P[l, :, PO["lam"]:PO["lam"] + NR] = fm(lam[l], NR)
        P[l, :, PO["gmix"]:PO["gmix"] + 8] = fm(g_mix[l], 8)
        P[l, :, PO["gmlp"]:PO["gmlp"] + 8] = fm(g_mlp[l], 8)
        P[l, :, PO["gfin"]:PO["gfin"] + 8] = fm(g_final, 8)
    return P


class V:
    __slots__ = ("ap", "keys")

    def __init__(self, ap, keys):
        self.ap = ap
        self.keys = tuple(keys)


class Prog:
    ENGS = ("pe", "act", "dve", "pool", "sp")

    def __init__(self, nc, es):
        self.nc = nc
        self.es = es
        self.ops = {e: [] for e in self.ENGS}
        self.sem = {}
        self.semval = {}
        for e in self.ENGS:
            self._mksem("E_" + e)
        self.waited = {e: {} for e in self.ENGS}
        self.last_w = {}
        self.readers = {}
        self.nops = {e: 0 for e in self.ENGS}
        self.nwaits = 0

    def _mksem(self, name):
        self.sem[name] = self.es.enter_context(self.nc.semaphore(name))
        self.semval[name] = 0

    def dma_sem(self, name):
        self._mksem(name)
        return name

    def op(self, eng, fn, ins=(), outs=(), dsem=None, nincs=1):
        deps = []
        for v in ins:
            for k in v.keys:
                t = self.last_w.get(k)
                if t is not None:
                    deps.append((t, True))
        for v in outs:
            for k in v.keys:
                t = self.last_w.get(k)
                if t is not None:
                    deps.append((t, False))
                for r in self.readers.get(k, ()):
                    deps.append((r, False))
        pos = self.nops[eng]
        waits = []
        wd = self.waited[eng]
        for (sname, val, seng, spos, is_dma), raw in deps:
            if seng == eng and not is_dma and dsem is None:
                if eng == "pe" or not raw or pos - spos > 2:
                    continue
            if wd.get(sname, 0) >= val:
                continue
            wd[sname] = val
            waits.append((sname, val))
        if dsem is None:
            sname = "E_" + eng
            self.semval[sname] += 1
            inc = 1
        else:
            sname = dsem
            self.semval[sname] += 16 * nincs
            inc = 16
        assert self.semval[sname] < 65000, (sname, self.semval[sname])
        tok = (sname, self.semval[sname], eng, pos, dsem is not None)
        self.nops[eng] += 1
        for v in ins:
            for k in v.keys:
                self.readers.setdefault(k, []).append(tok)
        for v in outs:
            for k in v.keys:
                self.last_w[k] = tok
                self.readers[k] = []
        self.nwaits += len(waits)
        self.ops[eng].append((waits, fn, sname, inc))
        return tok

    def wait_all(self, eng, toks):
        waits = []
        for (sname, val, _e, _p, _d) in toks:
            if self.waited[eng].get(sname, 0) < val:
                self.waited[eng][sname] = val
                waits.append((sname, val))
        self.ops[eng].append((waits, None, None, 0))

    def emit(self):
        nc = self.nc
        with nc.Block() as block:
            def run(e, lst):
                for waits, fn, sname, inc in lst:
                    for (s, v) in waits:
                        e.wait_ge(self.sem[s], v)
                    if fn is None:
                        continue
                    r = fn(e)
                    if isinstance(r, (list, tuple)):
                        for i in r:
                            i.then_inc(self.sem[sname], inc)
                    else:
                        r.then_inc(self.sem[sname], inc)

            @block.tensor
            def _(e):
                run(e, self.ops["pe"])

            @block.scalar
            def _(e):
                run(e, self.ops["act"])

            @block.vector
            def _(e):
                run(e, self.ops["dve"])

            @block.gpsimd
            def _(e):
                run(e, self.ops["pool"])

            @block.sync
            def _(e):
                run(e, self.ops["sp"])


def build_nc(NSEQ, NT, L, NSLOT=4, use_pow=False, dve_relu2=False):
    NU = (N_WT + WT_PER_UNIT - 1) // WT_PER_UNIT
    nc = bass.Bass("TRN2", target_bir_lowering=False)
    x_d = nc.dram_tensor("xT", [NSEQ, NT, 128, NC8, T], F32, kind="ExternalInput").ap()
    w_d = nc.dram_tensor("wts", [L, NU, 128, UNIT], F32, kind="ExternalInput").ap()
    p_d = nc.dram_tensor("prm", [L, 128, NP], F32, kind="ExternalInput").ap()
    o_d = nc.dram_tensor("outT", [NSEQ, NT, 128, NC8, T], F32, kind="ExternalOutput").ap()
    wbf_d = nc.dram_tensor("wbf", [L, NU, 128, UNIT], BF16).ap()

    es = ExitStack()
    with es:
        P = Prog(nc, es)

        def sb(name, shape, dt):
            return es.enter_context(nc.sbuf_tensor(name, shape, dt))

        xres = sb("xres", [128, NC8, T], F32)
        hbf = sb("hbf", [128, NC8, T], BF16)
        ring = sb("ring", [128, NSLOT, UNIT], BF16)
        prm = sb("prm_sb", [128, L, NP], F32)
        ones = sb("ones", [128, 128], BF16)
        sq = sb("sq", [128, 2, T], BF16)
        st_rstd = sb("st_rstd", [128, T], F32)
        st_a = sb("st_a", [128, T], F32)
        st_b = sb("st_b", [128, T], F32)
        tmpA = sb("tmpA", [128, 4, T], F32)
        ubuf = sb("ubuf", [128, NC8, T + KA - 1], BF16)
        cbq = sb("cbq", [128, 2, 2, T], BF16)
        arena = sb("arena", [128, 16, T], F32)
        vg = sb("vg", [128, NR, T], BF16)
        xbuf = sb("xbuf", [128, 3, T + KB - 1], F32)
        vv = sb("vv", [128, 2, 3, T], F32)
        vb = sb("vb", [128, 3, T], BF16)
        rg = sb("rg", [128, 3, 4, T], F32)
        ge = sb("ge", [128, 3, 2, T], F32)
        cpow = sb("cpow", [128, 2, T if use_pow else 2], F32)
        hist_u = sb("hist_u", [128, L, NC8, KA - 1], BF16)
        hist_x = sb("hist_x", [128, L, NR, KB - 1], F32)
        hst = sb("hst", [128, L, NR], F32)
        banks = [es.enter_context(nc.psum_tensor(f"bank{i}", [128, T], F32)) for i in range(8)]

        arena_bf = arena[:, :, :].bitcast(BF16)

        def V_x(c):
            return V(xres[:, c, :], [("xres", c)])

        def V_h(c):
            return V(hbf[:, c, :], [("hbf", c)])

        V_h_all = V(None, [("hbf", c) for c in range(NC8)])

        def V_cA(c):
            return V(arena[:, c, :], [("ar", c)])

        def V_ua(c):
            return V(arena_bf[:, 8 + c // 2, (c % 2) * T:(c % 2 + 1) * T], [("ar", 8 + c // 2)])

        def V_m(c):
            return V(arena_bf[:, 12 + c // 2, (c % 2) * T:(c % 2 + 1) * T], [("ar", 12 + c // 2)])

        def V_f(fc):
            return V(arena_bf[:, fc // 2, (fc % 2) * T:(fc % 2 + 1) * T], [("ar", fc // 2)])

        def V_vg(j):
            return V(vg[:, j, :], [("vg", j)])

        def V_bank(b):
            return V(banks[b][:, :], [("bank", b)])

        def V_prm(l):
            return V(None, [("prm", l)])

        def pcol(l, name, i=0):
            o = PO[name] + i
            return prm[:, l, o:o + 1]

        V_ones = V(ones[:, :], [("ones",)])

        def V_slot(i):
            return V(ring[:, i, :], [("slot", i)])

        s_prm = P.dma_sem("D_prm")
        s_x = P.dma_sem("D_x")
        s_o = P.dma_sem("D_o")
        s_slot = [P.dma_sem(f"D_slot{i}") for i in range(NSLOT)]
        s_stg = [P.dma_sem(f"D_stg{i}") for i in range(2)]
        s_bfo = [P.dma_sem(f"D_bfo{i}") for i in range(NSLOT)]

        P.op("pool", lambda e: e.memset(ones[:, :], 1.0 / D), outs=[V_ones])
        V_prm_all = V(None, [("prm", l) for l in range(L)])

        def ld_prm(e):
            return [e.dma_start(out=prm[:, l, :], in_=p_d[l]) for l in range(L)]
        P.op("sp", ld_prm, outs=[V_prm_all], dsem=s_prm, nincs=L)

        for l in range(L):
            lamv = prm[:, l, PO["lam"]:PO["lam"] + NR]
            ev = prm[:, l, PO["clam2"]:PO["clam2"] + NR]
            tv = prm[:, l, PO["tmp"]:PO["tmp"] + NR]
            cl = prm[:, l, PO["clam"]:PO["clam"] + NR]
            pv = [V_prm(l)]
            P.op("act", lambda e, o=ev, i=lamv: e.activation(out=o, in_=i, func=AF.Exp, scale=-1.0), ins=pv, outs=pv)
            P.op("dve", lambda e, o=tv, i=ev: e.tensor_scalar(out=o, in0=i, scalar1=-1.0 / 3.0, scalar2=0.5, op0=ALU.mult, op1=ALU.add), ins=pv, outs=pv)
            P.op("dve", lambda e, o=tv, i=ev: e.tensor_tensor(out=o, in0=o, in1=i, op=ALU.mult), ins=pv, outs=pv)
            P.op("dve", lambda e, o=tv: e.tensor_scalar(out=o, in0=o, scalar1=-1.0, scalar2=1.0, op0=ALU.mult, op1=ALU.add), ins=pv, outs=pv)
            P.op("dve", lambda e, o=tv, i=ev: e.tensor_tensor(out=o, in0=o, in1=i, op=ALU.mult), ins=pv, outs=pv)
            P.op("dve", lambda e, o=cl, i=tv: e.tensor_scalar(out=o, in0=i, scalar1=-8.0, scalar2=None, op0=ALU.mult), ins=pv, outs=pv)
            P.op("dve", lambda e, o=ev, i=cl: e.tensor_scalar(out=o, in0=i, scalar1=2.0, scalar2=None, op0=ALU.mult), ins=pv, outs=pv)
            for dst, src, w in (("hb_in", "b_in", 56), ("hbra", "bra", NR), ("hbrx", "brx", NR), ("hclam", "clam", NR)):
                P.op("dve", lambda e, o=prm[:, l, PO[dst]:PO[dst] + w], i=prm[:, l, PO[src]:PO[src] + w]:
                     e.tensor_scalar(out=o, in0=i, scalar1=0.5, scalar2=None, op0=ALU.mult), ins=pv, outs=pv)

        units = [(l, u) for l in range(L) for u in range(NU)]
        NUT = len(units)

        def V_stg(b):
            return V(arena[:, 8 * b:8 * b + 8, :], [("ar", 8 * b + i) for i in range(8)])

        def pp_in(n):
            l, u = units[n]
            b = n % 2
            P.op("sp", lambda e, o=arena[:, 8 * b:8 * b + 8, :], i=w_d[l, u]:
                 e.dma_start(out=o, in_=i.rearrange("p (a t) -> p a t", a=8)),
                 outs=[V_stg(b)], dsem=s_stg[b])

        def pp_cast(n):
            b = n % 2
            sl = n % NSLOT
            o = ring[:, sl, :].rearrange("p (a t) -> p a t", a=8)
            i = arena[:, 8 * b:8 * b + 8, :]
            ce = ("act", "dve", "pool")[n % 3]
            if ce == "act":
                fn = lambda e, o=o, i=i: e.activation(out=o, in_=i, func=AF.Copy)
            else:
                fn = lambda e, o=o, i=i: e.tensor_copy(out=o, in_=i)
            P.op(ce, fn, ins=[V_stg(b)], outs=[V_slot(sl)])

        def pp_out(n):
            l, u = units[n]
            sl = n % NSLOT
            P.op("sp", lambda e, o=wbf_d[l, u], i=ring[:, sl, :]: e.dma_start(out=o, in_=i),
                 ins=[V_slot(sl)], outs=[V(None, [("wbf", l, u)])], dsem=s_bfo[sl])

        pp_in(0)
        if NUT > 1:
            pp_in(1)
        for n in range(NUT):
            pp_cast(n)
            if n + 2 < NUT:
                pp_in(n + 2)
            pp_out(n)

        useq = [(l, u) for _s in range(NSEQ) for _t in range(NT) for l in range(L) for u in range(NU)]
        ws = {"emitted": 0, "gbase": 0}

        def w_ensure(g_hi):
            g_hi = min(g_hi, len(useq) - 1)
            while ws["emitted"] <= g_hi:
                g = ws["emitted"]
                l, u = useq[g]
                sl = g % NSLOT
                P.op("sp", lambda e, o=ring[:, sl, :], i=wbf_d[l, u]: e.dma_start(out=o, in_=i),
                     ins=[V(None, [("wbf", l, u)])], outs=[V_slot(sl)], dsem=s_slot[sl])
                ws["emitted"] += 1

        def wtile(n):
            g = ws["gbase"] + n // WT_PER_UNIT
            sl = g % NSLOT
            p = n % WT_PER_UNIT
            return ring[:, sl, p * 128:(p + 1) * 128], sl

        bank_rr = {"n": 0}

        def next_bank():
            b = 2 + bank_rr["n"] % 6
            bank_rr["n"] += 1
            return b

        def pe_group(n0, rhs_list, rhs_views, bank):
            g0 = ws["gbase"] + n0 // WT_PER_UNIT
            w_ensure(g0 + NSLOT - 1)
            nk = len(rhs_list)
            lts = [wtile(n0 + k) for k in range(nk)]
            slots = sorted(set(s for _, s in lts))
            out = banks[bank][:, :]

            def fn(e, lts=lts, rhs_list=rhs_list, out=out, nk=nk):
                r = None
                for k in range(nk):
                    r = e.matmul(out, lts[k][0], rhs_list[k], start=(k == 0), stop=(k == nk - 1))
                return r
            P.op("pe", fn, ins=[V_slot(s) for s in slots] + list(rhs_views), outs=[V_bank(bank)])
            return n0 + nk

        def act(out, in_, func, ins, outs, bias=None, scale=None):
            kw = {}
            if bias is not None:
                kw["bias"] = bias
            if scale is not None:
                kw["scale"] = scale
            P.op("act", lambda e, o=out, i=in_, f=func, kw=kw: e.activation(out=o, in_=i, func=f, **kw),
                 ins=ins, outs=outs)

        def tt(eng, out, in0, in1, op, ins, outs):
            P.op(eng, lambda e, o=out, a=in0, b=in1, op=op: e.tensor_tensor(out=o, in0=a, in1=b, op=op),
                 ins=ins, outs=outs)

        def ts(eng, out, in0, s1, s2, op0, op1, ins, outs):
            if s2 is None:
                P.op(eng, lambda e, o=out, a=in0, s1=s1, op0=op0: e.tensor_scalar(out=o, in0=a, scalar1=s1, scalar2=None, op0=op0),
                     ins=ins, outs=outs)
            else:
                P.op(eng, lambda e, o=out, a=in0, s1=s1, s2=s2, op0=op0, op1=op1:
                     e.tensor_scalar(out=o, in0=a, scalar1=s1, scalar2=s2, op0=op0, op1=op1),
                     ins=ins, outs=outs)

        def stt(eng, out, in0, scalar, in1, op0, op1, ins, outs):
            P.op(eng, lambda e, o=out, a=in0, s=scalar, b=in1, op0=op0, op1=op1:
                 e.scalar_tensor_tensor(out=o, in0=a, scalar=s, in1=b, op0=op0, op1=op1),
                 ins=ins, outs=outs)

        V_st_a = V(st_a[:, :], [("st_a",)])
        V_st_b = V(st_b[:, :], [("st_b",)])
        V_rstd = V(st_rstd[:, :], [("st_rstd",)])

        def rsqrt_into(out_ap, Vout, src_ap, Vsrc):
            if use_pow:
                tt("pool", out_ap, src_ap, cpow[:, 1, :], ALU.pow, [Vsrc, V_cpow], [Vout])
            else:
                act(src_ap, src_ap, AF.Sqrt, [Vsrc], [Vsrc])
                P.op("dve", lambda e, o=out_ap, i=src_ap: e.reciprocal(out=o, in_=i), ins=[Vsrc], outs=[Vout])

        def rms_stats():
            for c in range(NC8):
                Vsq = V(sq[:, c % 2, :], [("sq", c % 2)])
                act(sq[:, c % 2, :], xres[:, c, :], AF.Square, [V_x(c)], [Vsq])
                P.op("pe", lambda e, c=c: e.matmul(banks[0][:, :], ones[:, :], sq[:, c % 2, :],
                                                   start=(c == 0), stop=(c == NC8 - 1)),
                     ins=[V_ones, Vsq], outs=[V_bank(0)])
            ts("dve", st_a[:, :], banks[0][:, :], EPS, None, ALU.add, None, [V_bank(0)], [V_st_a])
            rsqrt_into(st_rstd[:, :], V_rstd, st_a[:, :], V_st_a)
            return V_rstd

        def layer(l, first):
            wn = 0
            Vp = V_prm(l)
            hk = [hbf[:, k, :] for k in range(NC8)]
            Vr = rms_stats()
            for c in range(NC8):
                stt("dve", hbf[:, c, :], xres[:, c, :], pcol(l, "gmix", c), st_rstd[:, :], ALU.mult, ALU.mult,
                    [V_x(c), Vr, Vp], [V_h(c)])

            def proj_a(c, wn):
                bA = next_bank()
                wn = pe_group(wn, hk, [V_h_all], bA)
                bG = next_bank()
                wn = pe_group(wn, hk, [V_h_all], bG)
                sgc = tmpA[:, c % 2, :]
                vabc = tmpA[:, 2 + c % 2, :]
                Vsg = V(sgc, [("tmpA", c % 2)])
                Vvab = V(vabc, [("tmpA", 2 + c % 2)])
                Vub = V(ubuf[:, c, :], [("ubuf", c)])
                Vhu = V(hist_u[:, l, c, :], [("hist_u", l, c)])
                act(sgc, banks[bG][:, :], AF.Tanh, [V_bank(bG), Vp], [Vsg],
                    scale=0.5, bias=pcol(l, "hb_in", 8 + c))
                act(vabc, banks[bA][:, :], AF.Identity, [V_bank(bA), Vp], [Vvab], bias=pcol(l, "b_in", c))
                P.op("pool", lambda e, o=ubuf[:, c, 0:KA - 1], i=hist_u[:, l, c, :]: e.tensor_copy(out=o, in_=i),
                     ins=[Vhu], outs=[Vub])
                stt("dve", ubuf[:, c, KA - 1:], sgc, 1.0, vabc, ALU.add, ALU.mult,
                    [Vsg, Vvab], [Vub])
                P.op("pool", lambda e, o=hist_u[:, l, c, :], i=ubuf[:, c, T:T + KA - 1]: e.tensor_copy(out=o, in_=i),
                     ins=[Vub], outs=[Vhu])
                return wn

            conv_bank = {}

            def conv_a(c, wn):
                kp = KP if c < 4 else KA
                Vub = V(ubuf[:, c, :], [("ubuf", c)])
                Vc = V_cA(c)
                bC = next_bank()
                wn = pe_group(wn, [ubuf[:, c, k:k + T] for k in range(kp)], [Vub], bC)
                Vcb = V(cbq[:, c % 2, 0, :], [("cbq", c % 2, 0)])
                Vsq = V(cbq[:, c % 2, 1, :], [("cbq", c % 2, 1)])
                cb = pcol(l, "cab", c)
                if kp == KA:
                    src, Vsrc = banks[bC][:, :], V_bank(bC)
                else:
                    cw = PO["caw"] + c * KA
                    ac = arena[:, c, :]
                    ts("dve", ac, ubuf[:, c, kp:kp + T], prm[:, l, cw + kp:cw + kp + 1], None, ALU.mult, None,
                       [Vub, Vp], [Vc])
                    for k in range(kp + 1, KA):
                        stt("dve", ac, ubuf[:, c, k:k + T], prm[:, l, cw + k:cw + k + 1], ac, ALU.mult, ALU.add,
                            [Vub, Vp, Vc], [Vc])
                    tt("dve", ac, banks[bC][:, :], ac, ALU.add, [V_bank(bC), Vc], [Vc])
                    src, Vsrc = ac, Vc
                act(cbq[:, c % 2, 0, :], src, AF.Identity, [Vsrc, Vp], [Vcb], scale=0.5, bias=cb)
                act(cbq[:, c % 2, 1, :], src, AF.Square, [Vsrc, Vp], [Vsq], scale=0.5, bias=cb)
                act(arena[:, c, :], src, AF.Identity, [Vsrc, Vp], [Vc], scale=0.5, bias=cb)
                return wn

            def ln_stats(c):
                Vcb = V(cbq[:, c % 2, 0, :], [("cbq", c % 2, 0)])
                Vsq = V(cbq[:, c % 2, 1, :], [("cbq", c % 2, 1)])
                P.op("pe", lambda e, c=c: e.matmul(banks[0][:, :], ones[:, :], cbq[:, c % 2, 0, :],
                                                   start=(c == 0), stop=(c == NC8 - 1)),
                     ins=[V_ones, Vcb], outs=[V_bank(0)])
                P.op("pe", lambda e, c=c: e.matmul(banks[1][:, :], ones[:, :], cbq[:, c % 2, 1, :],
                                                   start=(c == 0), stop=(c == NC8 - 1)),
                     ins=[V_ones, Vsq], outs=[V_bank(1)])

            def ln_finish():
                act(st_a[:, :], banks[0][:, :], AF.Square, [V_bank(0)], [V_st_a])
                stt("dve", st_a[:, :], banks[1][:, :], EPS, st_a[:, :], ALU.add, ALU.subtract,
                    [V_bank(1), V_st_a], [V_st_a])
                rsqrt_into(st_rstd[:, :], V_rstd, st_a[:, :], V_st_a)
                tt("dve", st_b[:, :], banks[0][:, :], st_rstd[:, :], ALU.mult, [V_bank(0), V_rstd], [V_st_b])

            def ln_norm(c):
                Vc = V_cA(c)
                eng = "dve" if c % 2 == 0 else "pool"
                tt(eng, arena[:, c, :], arena[:, c, :], st_rstd[:, :], ALU.mult, [Vc, V_rstd], [Vc])
                tt(eng, arena[:, c, :], arena[:, c, :], st_b[:, :], ALU.subtract, [Vc, V_st_b], [Vc])
                act(V_ua(c).ap, arena[:, c, :], AF.Silu, [Vc, Vp], [V_ua(c)],
                    scale=pcol(l, "lng", c), bias=pcol(l, "lnb", c))

            for c in range(NC8):
                wn = proj_a(c, wn)
                if 1 <= c <= 4:
                    wn = conv_a(c - 1, wn)
                if 2 <= c <= 5:
                    ln_stats(c - 2)

            RRs = {}

            def b_front(g, wn):
                for j in range(3):
                    jj = 3 * g + j
                    b = next_bank()
                    wn = pe_group(wn, hk, [V_h_all], b)
                    Vxb = V(xbuf[:, j, :], [("xbuf", j)])
                    Vhx = V(hist_x[:, l, jj, :], [("hist_x", l, jj)])
                    Vv = V(vv[:, g % 2, j, :], [("vv", g % 2, j)])
                    Vvb = V(vb[:, j, :], [("vb", j)])
                    vvj = vv[:, g % 2, j, :]
                    P.op("pool", lambda e, o=xbuf[:, j, 0:KB - 1], i=hist_x[:, l, jj, :]: e.tensor_copy(out=o, in_=i),
                         ins=[Vhx], outs=[Vxb])
                    act(xbuf[:, j, KB - 1:], banks[b][:, :], AF.Identity, [V_bank(b), Vp], [Vxb],
                        bias=pcol(l, "b_in", 16 + jj))
                    P.op("pool", lambda e, o=hist_x[:, l, jj, :], i=xbuf[:, j, T:T + KB - 1]: e.tensor_copy(out=o, in_=i),
                         ins=[Vxb], outs=[Vhx])
                    cw = PO["cbw"] + jj * KB
                    ts("dve", vvj, xbuf[:, j, 0:T], prm[:, l, cw:cw + 1], pcol(l, "cbb", jj), ALU.mult, ALU.add,
                       [Vxb, Vp], [Vv])
                    for k in range(1, KB):
                        stt("dve", vvj, xbuf[:, j, k:k + T], prm[:, l, cw + k:cw + k + 1], vvj,
                            ALU.mult, ALU.add, [Vxb, Vp, Vv], [Vv])
                    P.op("pool", lambda e, o=vb[:, j, :], i=vvj: e.tensor_copy(out=o, in_=i), ins=[Vv], outs=[Vvb])
                if g == 2:
                    ln_finish()
                if g >= 2:
                    for c in range(4 * (g - 2), 4 * (g - 2) + 4):
                        ln_norm(c)
                return wn

            def b_mid(g, wn):
                RR = {}
                for j in range(3):
                    jj = 3 * g + j
                    deps = BD_DEPS[j]
                    Vvbs = [V(vb[:, ci, :], [("vb", ci)]) for ci in deps]
                    bR = next_bank()
                    wn = pe_group(wn, [vb[:, ci, :] for ci in deps], Vvbs, bR)
                    bI = next_bank()
                    wn = pe_group(wn, [vb[:, ci, :] for ci in deps], Vvbs, bI)
                    bGb = next_bank()
                    wn = pe_group(wn, hk, [V_h_all], bGb)
                    R = [rg[:, j, q, :] for q in range(4)]
                    VR = [V(R[q], [("rg", j, q)]) for q in range(4)]
                    RR[j] = (R, VR)
                    act(R[0], banks[bR][:, :], AF.Tanh, [V_bank(bR), Vp], [VR[0]], scale=0.5, bias=pcol(l, "hbra", jj))
                    act(R[1], banks[bI][:, :], AF.Tanh, [V_bank(bI), Vp], [VR[1]], scale=0.5, bias=pcol(l, "hbrx", jj))
                    act(R[2], R[0], AF.Exp, [VR[0], Vp], [VR[2]], scale=pcol(l, "hclam", jj), bias=pcol(l, "hclam", jj))
                    act(R[3], R[0], AF.Exp, [VR[0], Vp], [VR[3]], scale=pcol(l, "clam", jj), bias=pcol(l, "clam", jj))
                    ts("pool", R[3], R[3], -0.25, 0.25, ALU.mult, ALU.add, [VR[3]], [VR[3]])
                    G0 = ge[:, j, 0, :]
                    G1 = ge[:, j, 1, :]
                    VG0 = V(G0, [("ge", j, 0)])
                    VG1 = V(G1, [("ge", j, 1)])
                    hb = pcol(l, "hb_in", 28 + jj)
                    act(G0, banks[bGb][:, :], AF.Identity, [V_bank(bGb), Vp], [VG0], scale=0.5, bias=hb)
                    tt("pool", G1, G0, G0, ALU.mult, [VG0], [VG1])
                    ts("pool", G1, G1, 4.0 * GELU_C * GELU_K, GELU_K, ALU.mult, ALU.add, [VG1], [VG1])
                    tt("pool", G1, G1, G0, ALU.mult, [VG1, VG0], [VG1])
                    act(G1, G1, AF.Tanh, [VG1], [VG1])
                RRs[g] = RR
                return wn

            def b_back(g):
                RR = RRs[g]
                for j in range(3):
                    R, VR = RR[j]
                    act(R[3], R[3], AF.Sqrt, [VR[3]], [VR[3]])
                for j in range(3):
                    jj = 3 * g + j
                    R, VR = RR[j]
                    Vv = V(vv[:, g % 2, j, :], [("vv", g % 2, j)])
                    Vhs = V(hst[:, l, jj:jj + 1], [("hst", l, jj)])
                    G0 = ge[:, j, 0, :]
                    G1 = ge[:, j, 1, :]
                    VG0 = V(G0, [("ge", j, 0)])
                    VG1 = V(G1, [("ge", j, 1)])
                    if first:
                        P.op("pool", lambda e, o=rg[:, j, 3, 0:1]: e.memset(o, 0.5), ins=[VR[3]], outs=[VR[3]])
                    stt("dve", R[1], R[1], 1.0, vv[:, g % 2, j, :], ALU.add, ALU.mult, [VR[1], Vv], [VR[1]])
                    tt("dve", R[1], R[1], R[3], ALU.mult, [VR[1], VR[3]], [VR[1]])
                    P.op("dve", lambda e, o=R[0], a=R[2], b=R[1], h=hst[:, l, jj:jj + 1]:
                         e.tensor_tensor_scan(out=o, data0=a, data1=b, initial=h, op0=ALU.mult, op1=ALU.add),
                         ins=[VR[2], VR[1], Vhs], outs=[VR[0]])
                    P.op("pool", lambda e, o=hst[:, l, jj:jj + 1], i=rg[:, j, 0, T - 1:T]: e.tensor_copy(out=o, in_=i),
                         ins=[VR[0]], outs=[Vhs])
                    stt("dve", G0, G1, 1.0, G0, ALU.add, ALU.mult, [VG1, VG0], [VG0])
                    tt("dve", vg[:, jj, :], R[0], G0, ALU.mult, [VR[0], VG0], [V_vg(jj)])

            wn = b_front(0, wn)
            for g in range(4):
                wn = b_mid(g, wn)
                if g < 2:
                    wn = conv_a(4 + 2 * g, wn)
                    ln_stats(4 + 2 * g)
                    wn = conv_a(5 + 2 * g, wn)
                    ln_stats(5 + 2 * g)
                if g + 1 < 4:
                    wn = b_front(g + 1, wn)
                b_back(g)

            V_ua_all = V(None, [("ar", 8 + i) for i in range(4)])
            V_vg_all = V(None, [("vg", j) for j in range(NR)])
            def sasb(co):
                nonlocal wn
                p = co % 2
                bSa = next_bank()
                bSb = next_bank()
                wn = pe_group(wn, hk, [V_h_all], bSa)
                wn = pe_group(wn, hk, [V_h_all], bSb)
                VM0 = V(tmpA[:, 2 * p, :], [("tmpA", 2 * p)])
                VM1 = V(tmpA[:, 2 * p + 1, :], [("tmpA", 2 * p + 1)])
                act(tmpA[:, 2 * p, :], banks[bSa][:, :], AF.Tanh, [V_bank(bSa), Vp], [VM0], scale=0.5, bias=pcol(l, "hb_in", 40 + co))
                act(tmpA[:, 2 * p + 1, :], banks[bSb][:, :], AF.Tanh, [V_bank(bSb), Vp], [VM1], scale=0.5, bias=pcol(l, "hb_in", 48 + co))

            def yayb(co):
                nonlocal wn
                p = co % 2
                bYa = next_bank()
                bYb = next_bank()
                wn = pe_group(wn, [V_ua(k).ap for k in range(NC8)], [V_ua_all], bYa)
                wn = pe_group(wn, [vg[:, j, :] for j in range(NR)], [V_vg_all], bYb)
                M0 = tmpA[:, 2 * p, :]
                M1 = tmpA[:, 2 * p + 1, :]
                VM0 = V(M0, [("tmpA", 2 * p)])
                VM1 = V(M1, [("tmpA", 2 * p + 1)])
                stt("dve", M0, M0, 1.0, banks[bYa][:, :], ALU.add, ALU.mult, [V_bank(bYa), VM0], [VM0])
                stt("dve", M1, M1, 1.0, banks[bYb][:, :], ALU.add, ALU.mult, [V_bank(bYb), VM1], [VM1])
                tt("pool", V_m(co).ap, M0, M1, ALU.add, [VM0, VM1], [V_m(co)])

            sasb(0)
            sasb(1)
            for co in range(NC8):
                yayb(co)
                if co + 2 < NC8:
                    sasb(co + 2)

            V_m_all = V(None, [("ar", 12 + i) for i in range(4)])
            for co in range(NC8):
                b = next_bank()
                wn = pe_group(wn, [V_m(k).ap for k in range(NC8)], [V_m_all], b)
                stt("dve", xres[:, co, :], banks[b][:, :], 0.5, xres[:, co, :], ALU.mult, ALU.add,
                    [V_bank(b), V_x(co)], [V_x(co)])

            Vr = rms_stats()
            for c in range(NC8):
                stt("dve", hbf[:, c, :], xres[:, c, :], pcol(l, "gmlp", c), st_rstd[:, :], ALU.mult, ALU.mult,
                    [V_x(c), Vr, Vp], [V_h(c)])
            for fc in range(NF):
                b = next_bank()
                wn = pe_group(wn, hk, [V_h_all], b)
                if dve_relu2:
                    stt("dve", V_f(fc).ap, banks[b][:, :], 0.0, banks[b][:, :], ALU.max, ALU.mult,
                        [V_bank(b)], [V_f(fc)])
                else:
                    rlc = tmpA[:, fc % 4, :]
                    Vrl = V(rlc, [("tmpA", fc % 4)])
                    act(rlc, banks[b][:, :], AF.Relu, [V_bank(b)], [Vrl])
                    stt("dve", V_f(fc).ap, rlc, 0.0, banks[b][:, :], ALU.add, ALU.mult,
                        [Vrl, V_bank(b)], [V_f(fc)])
            V_f_all = V(None, [("ar", i) for i in range(16)])
            for co in range(NC8):
                b = next_bank()
                wn = pe_group(wn, [V_f(fc).ap for fc in range(NF)], [V_f_all], b)
                tt("dve", xres[:, co, :], banks[b][:, :], xres[:, co, :], ALU.add, [V_bank(b), V_x(co)], [V_x(co)])
            assert wn == N_WT, wn
            ws["gbase"] += NU

        cst = sb("cst", [128, 2], F32)
        V_cst = V(None, [("cst",)])
        P.op("pool", lambda e: e.memset(cst[:, 0:1], EPS), outs=[V_cst])
        P.op("pool", lambda e: e.memset(cst[:, 1:2], 1.0), outs=[V_cst])
        V_cpow = V(None, [("cpow",)])
        P.op("pool", lambda e: e.memset(cpow[:, 0, :], 0.5), outs=[V_cpow])
        P.op("pool", lambda e: e.memset(cpow[:, 1, :], -0.5), outs=[V_cpow])

        out_toks = []
        V_x_all = V(None, [("xres", c) for c in range(NC8)])
        V_out = V(None, [("ar", c) for c in range(NC8)])
        for s in range(NSEQ):
            P.op("pool", lambda e: e.memset(hist_u[:, :, :, :], 0.0),
                 outs=[V(None, [("hist_u", l, c) for l in range(L) for c in range(NC8)])])
            P.op("pool", lambda e: e.memset(hist_x[:, :, :, :], 0.0),
                 outs=[V(None, [("hist_x", l, j) for l in range(L) for j in range(NR)])])
            P.op("pool", lambda e: e.memset(hst[:, :, :], 0.0),
                 outs=[V(None, [("hst", l, j) for l in range(L) for j in range(NR)])])
            for t in range(NT):
                P.op("pool", lambda e, s=s, t=t: e.dma_start(out=xres[:, :, :], in_=x_d[s, t]),
                     outs=[V_x_all], dsem=s_x)
                for l in range(L):
                    layer(l, t == 0)
                Vr = rms_stats()
                for c in range(NC8):
                    stt("dve", arena[:, c, :], xres[:, c, :], pcol(0, "gfin", c), st_rstd[:, :], ALU.mult, ALU.mult,
                        [V_x(c), Vr, V_prm(0)], [V_cA(c)])
                tok = P.op("pool", lambda e, s=s, t=t: e.dma_start(out=o_d[s, t], in_=arena[:, 0:NC8, :]),
                           ins=[V_out], outs=[V(None, [("out", s, t)])], dsem=s_o)
                out_toks.append(tok)
        P.wait_all("sp", out_toks)
        P.wait_all("pool", out_toks)
        P.emit()
        build_nc.stats = dict(nops=dict(P.nops), nwaits=P.nwaits)
    return nc


N_CORES = 8
_CACHE = {}


def _prep(inputs, L):
    w = _pack_weights(L, w_in=inputs["w_in"], w_a_out=inputs["w_a_out"], w_b_out=inputs["w_b_out"],
                      w_o=inputs["w_o"], w_1=inputs["w_1"], w_2=inputs["w_2"],
                      w_rg_a=inputs["w_rg_a"], w_rg_x=inputs["w_rg_x"], conv_a_w=inputs["conv_a_w"])
    p = _pack_params(L, inputs["b_in"], inputs["conv_a_w"], inputs["conv_a_b"], inputs["ln_g"], inputs["ln_b"],
                     inputs["conv_b_w"], inputs["conv_b_b"], inputs["b_rg_a"], inputs["b_rg_x"], inputs["lam"],
                     inputs["g_mix"], inputs["g_mlp"], inputs["g_final"])
    return w, p


def _x_to_tiles(x):
    B, S, _ = x.shape
    return np.ascontiguousarray(x.reshape(B, S // T, T, NC8, 128).transpose(0, 1, 4, 3, 2))


def _tiles_to_x(o):
    B, NTt = o.shape[0], o.shape[1]
    return np.ascontiguousarray(o.transpose(0, 1, 4, 3, 2).reshape(B, NTt * T, D))


def run(inputs, L, n_cores, **bkw):
    inputs = {k: np.asarray(v, dtype=np.float32) for k, v in inputs.items()}
    x = inputs["x"]
    B, S, _ = x.shape
    nseq = B // n_cores
    NT = S // T
    w, p = _prep(inputs, L)
    xt = _x_to_tiles(x)
    key = (nseq, NT, L, tuple(sorted(bkw.items())))
    nc = build_nc(nseq, NT, L, **bkw)
    in_maps = [{"xT": xt[i * nseq:(i + 1) * nseq], "wts": w, "prm": p} for i in range(n_cores)]
    import os
    res = run_bass_kernel_spmd(nc, in_maps, core_ids=list(range(n_cores)), trace=bool(os.environ.get("K_TRACE")))
    run.last = res
    out = np.concatenate([r["outT"] for r in res.results], axis=0)
    return _tiles_to_x(out).astype(np.float32)


def kernel(**inputs):
    return run(inputs, L=4, n_cores=N_CORES)
```

```python
import numpy as np
from contextlib import ExitStack

import concourse.bass as bass
import concourse.mybir as mybir
from concourse.bass_utils import run_bass_kernel_spmd

F32 = mybir.dt.float32
BF16 = mybir.dt.bfloat16
AF = mybir.ActivationFunctionType
ALU = mybir.AluOpType

D = 1024
NC8 = 8
D_RNN = 1536
NR = 12
NF = 32
KA = 31
KB = 4
T = 512
EPS = 1e-6
WT_PER_UNIT = 32
UNIT = WT_PER_UNIT * 128
BD_DEPS = {0: (0, 1), 1: (0, 1, 2), 2: (1, 2)}
KP = KA
N_WT = 1240 + 4 * KP + 4 * KA
GELU_K = 1.5957691216057308
GELU_C = 0.044715

PO = {}
_o = 0
for _n, _w in (("b_in", 56), ("caw", 8 * KA), ("cab", 8), ("lng", 8), ("lnb", 8),
               ("cbw", NR * KB), ("cbb", NR), ("bra", NR), ("brx", NR), ("lam", NR),
               ("gmix", 8), ("gmlp", 8), ("gfin", 8), ("clam", NR), ("clam2", NR),
               ("tmp", NR), ("hb_in", 56), ("hbra", NR), ("hbrx", NR), ("hclam", NR)):
    PO[_n] = _o
    _o += _w
NP = _o


def _layer_wtiles(l, w_in, w_a_out, w_b_out, w_o, w_1, w_2, w_rg_a, w_rg_x, conv_a_w):
    tiles = []
    win = w_in[l]

    def col(w, c0, nk):
        for k in range(nk):
            tiles.append(w[k * 128:(k + 1) * 128, c0:c0 + 128])

    bd = []
    for w in (w_rg_a[l], w_rg_x[l]):
        m = np.zeros((D_RNN, D_RNN), np.float32)
        for h in range(16):
            m[h * 96:(h + 1) * 96, h * 96:(h + 1) * 96] = w[h]
        bd.append(m)
    def conv_tiles(c):
        for k in range(KP if c < 4 else KA):
            tiles.append(np.diag(conv_a_w[l][k, c * 128:(c + 1) * 128]))

    for c in range(8):
        col(win, 0 + c * 128, 8)
        col(win, 1024 + c * 128, 8)
        if 1 <= c <= 4:
            conv_tiles(c - 1)
    for g in range(4):
        for j in range(3):
            col(win, 2048 + (3 * g + j) * 128, 8)
        for j in range(3):
            for m in bd:
                for ci in BD_DEPS[j]:
                    tiles.append(m[(3 * g + ci) * 128:(3 * g + ci + 1) * 128,
                                   (3 * g + j) * 128:(3 * g + j + 1) * 128])
            col(win, 3584 + (3 * g + j) * 128, 8)
        if g < 2:
            conv_tiles(4 + 2 * g)
            conv_tiles(5 + 2 * g)
    def sasb(co):
        col(win, 5120 + co * 128, 8)
        col(win, 6144 + co * 128, 8)

    sasb(0)
    sasb(1)
    for co in range(8):
        col(w_a_out[l], co * 128, 8)
        col(w_b_out[l], co * 128, 12)
        if co + 2 < 8:
            sasb(co + 2)
    for co in range(8):
        col(w_o[l], co * 128, 8)
    for fc in range(NF):
        col(w_1[l], fc * 128, 8)
    for co in range(8):
        col(w_2[l], co * 128, 32)
    assert len(tiles) == N_WT
    return tiles


def _pack_weights(L, **w):
    nu = (N_WT + WT_PER_UNIT - 1) // WT_PER_UNIT
    out = np.zeros((L, nu, 128, UNIT), np.float32)
    for l in range(L):
        tiles = _layer_wtiles(l, **w)
        for i, t in enumerate(tiles):
            u, p = divmod(i, WT_PER_UNIT)
            out[l, u, :, p * 128:(p + 1) * 128] = t
    return out


def _pack_params(L, b_in, conv_a_w, conv_a_b, ln_g, ln_b, conv_b_w, conv_b_b,
                 b_rg_a, b_rg_x, lam, g_mix, g_mlp, g_final):
    P = np.zeros((L, 128, NP), np.float32)

    def fm(v, n):
        return np.ascontiguousarray(v.reshape(n, 128).T)

    for l in range(L):
        P[l, :, PO["b_in"]:PO["b_in"] + 56] = fm(b_in[l], 56)
        P[l, :, PO["caw"]:PO["caw"] + 8 * KA] = \
            conv_a_w[l].reshape(KA, 8, 128).transpose(2, 1, 0).reshape(128, 8 * KA)
        P[l, :, PO["cab"]:PO["cab"] + 8] = fm(conv_a_b[l], 8)
        P[l, :, PO["lng"]:PO["lng"] + 8] = fm(ln_g[l], 8)
        P[l, :, PO["lnb"]:PO["lnb"] + 8] = fm(ln_b[l], 8)
        P[l, :, PO["cbw"]:PO["cbw"] + NR * KB] = \
            conv_b_w[l].reshape(KB, NR, 128).transpose(2, 1, 0).reshape(128, NR * KB)
        P[l, :, PO["cbb"]:PO["cbb"] + NR] = fm(conv_b_b[l], NR)
        P[l, :, PO["bra"]:PO["bra"] + NR] = fm(b_rg_a[l], NR)
        P[l, :, PO["brx"]:PO["brx"] + NR] = fm(b_rg_x[l], NR)
        P[l, :, PO["lam"]:PO["lam"] + NR] = fm(lam[l], NR)
        P[l, :, PO["gmix"]:PO["gmix"] + 8] = fm(g_mix[l], 8)
        P[l, :, PO["gmlp"]:PO["gmlp"] + 8] = fm(g_mlp[l], 8)
        P[l, :, PO["gfin"]:PO["gfin"] + 8] = fm(g_final, 8)
    return P


class V:
    __slots__ = ("ap", "keys")

    def __init__(self, ap, keys):
        self.ap = ap
        self.keys = tuple(keys)


class Prog:
    ENGS = ("pe", "act", "dve", "pool", "sp")

    def __init__(self, nc, es):
        self.nc = nc
        self.es = es
        self.ops = {e: [] for e in self.ENGS}
        self.sem = {}
        self.semval = {}
        for e in self.ENGS:
            self._mksem("E_" + e)
        self.waited = {e: {} for e in self.ENGS}
        self.last_w = {}
        self.readers = {}
        self.nops = {e: 0 for e in self.ENGS}
        self.nwaits = 0

    def _mksem(self, name):
        self.sem[name] = self.es.enter_context(self.nc.semaphore(name))
        self.semval[name] = 0

    def dma_sem(self, name):
        self._mksem(name)
        return name

    def op(self, eng, fn, ins=(), outs=(), dsem=None, nincs=1):
        deps = []
        for v in ins:
            for k in v.keys:
                t = self.last_w.get(k)
                if t is not None:
                    deps.append((t, True))
        for v in outs:
            for k in v.keys:
                t = self.last_w.get(k)
                if t is not None:
                    deps.append((t, False))
                for r in self.readers.get(k, ()):
                    deps.append((r, False))
        pos = self.nops[eng]
        waits = []
        wd = self.waited[eng]
        for (sname, val, seng, spos, is_dma), raw in deps:
            if seng == eng and not is_dma and dsem is None:
                if eng == "pe" or not raw or pos - spos > 2:
                    continue
            if wd.get(sname, 0) >= val:
                continue
            wd[sname] = val
            waits.append((sname, val))
        if dsem is None:
            sname = "E_" + eng
            self.semval[sname] += 1
            inc = 1
        else:
            sname = dsem
            self.semval[sname] += 16 * nincs
            inc = 16
        assert self.semval[sname] < 65000, (sname, self.semval[sname])
        tok = (sname, self.semval[sname], eng, pos, dsem is not None)
        self.nops[eng] += 1
        for v in ins:
            for k in v.keys:
                self.readers.setdefault(k, []).append(tok)
        for v in outs:
            for k in v.keys:
                self.last_w[k] = tok
                self.readers[k] = []
        self.nwaits += len(waits)
        self.ops[eng].append((waits, fn, sname, inc))
        return tok

    def wait_all(self, eng, toks):
        waits = []
        for (sname, val, _e, _p, _d) in toks:
            if self.waited[eng].get(sname, 0) < val:
                self.waited[eng][sname] = val
                waits.append((sname, val))
        self.ops[eng].append((waits, None, None, 0))

    def emit(self):
        nc = self.nc
        with nc.Block() as block:
            def run(e, lst):
                for waits, fn, sname, inc in lst:
                    for (s, v) in waits:
                        e.wait_ge(self.sem[s], v)
                    if fn is None:
                        continue
                    r = fn(e)
                    if isinstance(r, (list, tuple)):
                        for i in r:
                            i.then_inc(self.sem[sname], inc)
                    else:
                        r.then_inc(self.sem[sname], inc)

            @block.tensor
            def _(e):
                run(e, self.ops["pe"])

            @block.scalar
            def _(e):
                run(e, self.ops["act"])

            @block.vector
            def _(e):
                run(e, self.ops["dve"])

            @block.gpsimd
            def _(e):
                run(e, self.ops["pool"])

            @block.sync
            def _(e):
                run(e, self.ops["sp"])


def build_nc(NSEQ, NT, L, NSLOT=4, use_pow=False, dve_relu2=False):
    NU = (N_WT + WT_PER_UNIT - 1) // WT_PER_UNIT
    nc = bass.Bass("TRN2", target_bir_lowering=False)
    x_d = nc.dram_tensor("xT", [NSEQ, NT, 128, NC8, T], F32, kind="ExternalInput").ap()
    w_d = nc.dram_tensor("wts", [L, NU, 128, UNIT], F32, kind="ExternalInput").ap()
    p_d = nc.dram_tensor("prm", [L, 128, NP], F32, kind="ExternalInput").ap()
    o_d = nc.dram_tensor("outT", [NSEQ, NT, 128, NC8, T], F32, kind="ExternalOutput").ap()
    wbf_d = nc.dram_tensor("wbf", [L, NU, 128, UNIT], BF16).ap()

    es = ExitStack()
    with es:
        P = Prog(nc, es)

        def sb(name, shape, dt):
            return es.enter_context(nc.sbuf_tensor(name, shape, dt))

        xres = sb("xres", [128, NC8, T], F32)
        hbf = sb("hbf", [128, NC8, T], BF16)
        ring = sb("ring", [128, NSLOT, UNIT], BF16)
        prm = sb("prm_sb", [128, L, NP], F32)
        ones = sb("ones", [128, 128], BF16)
        sq = sb("sq", [128, 2, T], BF16)
        st_rstd = sb("st_rstd", [128, T], F32)
        st_a = sb("st_a", [128, T], F32)
        st_b = sb("st_b", [128, T], F32)
        tmpA = sb("tmpA", [128, 4, T], F32)
        ubuf = sb("ubuf", [128, NC8, T + KA - 1], BF16)
        cbq = sb("cbq", [128, 2, 2, T], BF16)
        arena = sb("arena", [128, 16, T], F32)
        vg = sb("vg", [128, NR, T], BF16)
        xbuf = sb("xbuf", [128, 3, T + KB - 1], F32)
        vv = sb("vv", [128, 2, 3, T], F32)
        vb = sb("vb", [128, 3, T], BF16)
        rg = sb("rg", [128, 3, 4, T], F32)
        ge = sb("ge", [128, 3, 2, T], F32)
        cpow = sb("cpow", [128, 2, T if use_pow else 2], F32)
        hist_u = sb("hist_u", [128, L, NC8, KA - 1], BF16)
        hist_x = sb("hist_x", [128, L, NR, KB - 1], F32)
        hst = sb("hst", [128, L, NR], F32)
        banks = [es.enter_context(nc.psum_tensor(f"bank{i}", [128, T], F32)) for i in range(8)]

        arena_bf = arena[:, :, :].bitcast(BF16)

        def V_x(c):
            return V(xres[:, c, :], [("xres", c)])

        def V_h(c):
            return V(hbf[:, c, :], [("hbf", c)])

        V_h_all = V(None, [("hbf", c) for c in range(NC8)])

        def V_cA(c):
            return V(arena[:, c, :], [("ar", c)])

        def V_ua(c):
            return V(arena_bf[:, 8 + c // 2, (c % 2) * T:(c % 2 + 1) * T], [("ar", 8 + c // 2)])

        def V_m(c):
            return V(arena_bf[:, 12 + c // 2, (c % 2) * T:(c % 2 + 1) * T], [("ar", 12 + c // 2)])

        def V_f(fc):
            return V(arena_bf[:, fc // 2, (fc % 2) * T:(fc % 2 + 1) * T], [("ar", fc // 2)])

        def V_vg(j):
            return V(vg[:, j, :], [("vg", j)])

        def V_bank(b):
            return V(banks[b][:, :], [("bank", b)])

        def V_prm(l):
            return V(None, [("prm", l)])

        def pcol(l, name, i=0):
            o = PO[name] + i
            return prm[:, l, o:o + 1]

        V_ones = V(ones[:, :], [("ones",)])

        def V_slot(i):
            return V(ring[:, i, :], [("slot", i)])

        s_prm = P.dma_sem("D_prm")
        s_x = P.dma_sem("D_x")
        s_o = P.dma_sem("D_o")
        s_slot = [P.dma_sem(f"D_slot{i}") for i in range(NSLOT)]
        s_stg = [P.dma_sem(f"D_stg{i}") for i in range(3)]
        s_bfo = [P.dma_sem(f"D_bfo{i}") for i in range(NSLOT)]

        P.op("pool", lambda e: e.memset(ones[:, :], 1.0 / D), outs=[V_ones])
        V_prm_all = V(None, [("prm", l) for l in range(L)])

        def ld_prm(e):
            return [e.dma_start(out=prm[:, l, :], in_=p_d[l]) for l in range(L)]
        P.op("sp", ld_prm, outs=[V_prm_all], dsem=s_prm, nincs=L)

        for l in range(L):
            lamv = prm[:, l, PO["lam"]:PO["lam"] + NR]
            ev = prm[:, l, PO["clam2"]:PO["clam2"] + NR]
            tv = prm[:, l, PO["tmp"]:PO["tmp"] + NR]
            cl = prm[:, l, PO["clam"]:PO["clam"] + NR]
            pv = [V_prm(l)]
            P.op("act", lambda e, o=ev, i=lamv: e.activation(out=o, in_=i, func=AF.Exp, scale=-1.0), ins=pv, outs=pv)
            P.op("dve", lambda e, o=tv, i=ev: e.tensor_scalar(out=o, in0=i, scalar1=-1.0 / 3.0, scalar2=0.5, op0=ALU.mult, op1=ALU.add), ins=pv, outs=pv)
            P.op("dve", lambda e, o=tv, i=ev: e.tensor_tensor(out=o, in0=o, in1=i, op=ALU.mult), ins=pv, outs=pv)
            P.op("dve", lambda e, o=tv: e.tensor_scalar(out=o, in0=o, scalar1=-1.0, scalar2=1.0, op0=ALU.mult, op1=ALU.add), ins=pv, outs=pv)
            P.op("dve", lambda e, o=tv, i=ev: e.tensor_tensor(out=o, in0=o, in1=i, op=ALU.mult), ins=pv, outs=pv)
            P.op("dve", lambda e, o=cl, i=tv: e.tensor_scalar(out=o, in0=i, scalar1=-8.0, scalar2=None, op0=ALU.mult), ins=pv, outs=pv)
            P.op("dve", lambda e, o=ev, i=cl: e.tensor_scalar(out=o, in0=i, scalar1=2.0, scalar2=None, op0=ALU.mult), ins=pv, outs=pv)
            for dst, src, w in (("hb_in", "b_in", 56), ("hbra", "bra", NR), ("hbrx", "brx", NR), ("hclam", "clam", NR)):
                P.op("dve", lambda e, o=prm[:, l, PO[dst]:PO[dst] + w], i=prm[:, l, PO[src]:PO[src] + w]:
                     e.tensor_scalar(out=o, in0=i, scalar1=0.5, scalar2=None, op0=ALU.mult), ins=pv, outs=pv)

        units = [(l, u) for l in range(L) for u in range(NU)]
        NUT = len(units)

        stg_ap = [arena[:, 0:8, :], arena[:, 8:16, :], xres[:, :, :]]
        stg_keys = [[("ar", i) for i in range(8)], [("ar", 8 + i) for i in range(8)],
                    [("xres", c) for c in range(NC8)]]
        NSTG = 3

        def V_stg(b):
            return V(stg_ap[b], stg_keys[b])

        def pp_in(n):
            l, u = units[n]
            b = n % NSTG
            P.op("sp", lambda e, o=stg_ap[b], i=w_d[l, u]:
                 e.dma_start(out=o, in_=i.rearrange("p (a t) -> p a t", a=8)),
                 outs=[V_stg(b)], dsem=s_stg[b])

        def pp_cast(n):
            b = n % NSTG
            sl = n % NSLOT
            o = ring[:, sl, :].rearrange("p (a t) -> p a t", a=8)
            i = stg_ap[b]
            ce = ("act", "dve", "dve", "pool")[n % 4]
            if ce == "act":
                fn = lambda e, o=o, i=i: e.activation(out=o, in_=i, func=AF.Copy)
            else:
                fn = lambda e, o=o, i=i: e.tensor_copy(out=o, in_=i)
            P.op(ce, fn, ins=[V_stg(b)], outs=[V_slot(sl)])

        def pp_out(n):
            l, u = units[n]
            sl = n % NSLOT
            P.op("sp", lambda e, o=wbf_d[l, u], i=ring[:, sl, :]: e.dma_start(out=o, in_=i),
                 ins=[V_slot(sl)], outs=[V(None, [("wbf", l, u)])], dsem=s_bfo[sl])

        for n in range(min(NSTG, NUT)):
            pp_in(n)
        for n in range(NUT):
            pp_cast(n)
            if n + NSTG < NUT:
                pp_in(n + NSTG)
            pp_out(n)

        useq = [(l, u) for _s in range(NSEQ) for _t in range(NT) for l in range(L) for u in range(NU)]
        ws = {"emitted": 0, "gbase": 0}

        def w_ensure(g_hi):
            g_hi = min(g_hi, len(useq) - 1)
            while ws["emitted"] <= g_hi:
                g = ws["emitted"]
                l, u = useq[g]
                sl = g % NSLOT
                P.op("sp", lambda e, o=ring[:, sl, :], i=wbf_d[l, u]: e.dma_start(out=o, in_=i),
                     ins=[V(None, [("wbf", l, u)])], outs=[V_slot(sl)], dsem=s_slot[sl])
                ws["emitted"] += 1

        def wtile(n):
            g = ws["gbase"] + n // WT_PER_UNIT
            sl = g % NSLOT
            p = n % WT_PER_UNIT
            return ring[:, sl, p * 128:(p + 1) * 128], sl

        bank_rr = {"n": 0}

        def next_bank():
            b = 2 + bank_rr["n"] % 6
            bank_rr["n"] += 1
            return b

        def pe_group(n0, rhs_list, rhs_views, bank):
            g0 = ws["gbase"] + n0 // WT_PER_UNIT
            w_ensure(g0 + NSLOT - 1)
            nk = len(rhs_list)
            lts = [wtile(n0 + k) for k in range(nk)]
            slots = sorted(set(s for _, s in lts))
            out = banks[bank][:, :]

            def fn(e, lts=lts, rhs_list=rhs_list, out=out, nk=nk):
                r = None
                for k in range(nk):
                    r = e.matmul(out, lts[k][0], rhs_list[k], start=(k == 0), stop=(k == nk - 1))
                return r
            P.op("pe", fn, ins=[V_slot(s) for s in slots] + list(rhs_views), outs=[V_bank(bank)])
            return n0 + nk

        def act(out, in_, func, ins, outs, bias=None, scale=None):
            kw = {}
            if bias is not None:
                kw["bias"] = bias
            if scale is not None:
                kw["scale"] = scale
            P.op("act", lambda e, o=out, i=in_, f=func, kw=kw: e.activation(out=o, in_=i, func=f, **kw),
                 ins=ins, outs=outs)

        def tt(eng, out, in0, in1, op, ins, outs):
            P.op(eng, lambda e, o=out, a=in0, b=in1, op=op: e.tensor_tensor(out=o, in0=a, in1=b, op=op),
                 ins=ins, outs=outs)

        def ts(eng, out, in0, s1, s2, op0, op1, ins, outs):
            if s2 is None:
                P.op(eng, lambda e, o=out, a=in0, s1=s1, op0=op0: e.tensor_scalar(out=o, in0=a, scalar1=s1, scalar2=None, op0=op0),
                     ins=ins, outs=outs)
            else:
                P.op(eng, lambda e, o=out, a=in0, s1=s1, s2=s2, op0=op0, op1=op1:
                     e.tensor_scalar(out=o, in0=a, scalar1=s1, scalar2=s2, op0=op0, op1=op1),
                     ins=ins, outs=outs)

        def stt(eng, out, in0, scalar, in1, op0, op1, ins, outs):
            P.op(eng, lambda e, o=out, a=in0, s=scalar, b=in1, op0=op0, op1=op1:
                 e.scalar_tensor_tensor(out=o, in0=a, scalar=s, in1=b, op0=op0, op1=op1),
                 ins=ins, outs=outs)

        V_st_a = V(st_a[:, :], [("st_a",)])
        V_st_b = V(st_b[:, :], [("st_b",)])
        V_rstd = V(st_rstd[:, :], [("st_rstd",)])

        def rsqrt_into(out_ap, Vout, src_ap, Vsrc):
            if use_pow:
                tt("pool", out_ap, src_ap, cpow[:, 1, :], ALU.pow, [Vsrc, V_cpow], [Vout])
            else:
                act(src_ap, src_ap, AF.Sqrt, [Vsrc], [Vsrc])
                P.op("dve", lambda e, o=out_ap, i=src_ap: e.reciprocal(out=o, in_=i), ins=[Vsrc], outs=[Vout])

        def rms_stats():
            for c in range(NC8):
                Vsq = V(sq[:, c % 2, :], [("sq", c % 2)])
                act(sq[:, c % 2, :], xres[:, c, :], AF.Square, [V_x(c)], [Vsq])
                P.op("pe", lambda e, c=c: e.matmul(banks[0][:, :], ones[:, :], sq[:, c % 2, :],
                                                   start=(c == 0), stop=(c == NC8 - 1)),
                     ins=[V_ones, Vsq], outs=[V_bank(0)])
            ts("dve", st_a[:, :], banks[0][:, :], EPS, None, ALU.add, None, [V_bank(0)], [V_st_a])
            rsqrt_into(st_rstd[:, :], V_rstd, st_a[:, :], V_st_a)
            return V_rstd

        def layer(l, first):
            wn = 0
            Vp = V_prm(l)
            hk = [hbf[:, k, :] for k in range(NC8)]
            Vr = rms_stats()
            for c in range(NC8):
                stt("dve", hbf[:, c, :], xres[:, c, :], pcol(l, "gmix", c), st_rstd[:, :], ALU.mult, ALU.mult,
                    [V_x(c), Vr, Vp], [V_h(c)])

            def proj_a(c, wn):
                bA = next_bank()
                wn = pe_group(wn, hk, [V_h_all], bA)
                bG = next_bank()
                wn = pe_group(wn, hk, [V_h_all], bG)
                sgc = tmpA[:, c % 2, :]
                vabc = tmpA[:, 2 + c % 2, :]
                Vsg = V(sgc, [("tmpA", c % 2)])
                Vvab = V(vabc, [("tmpA", 2 + c % 2)])
                Vub = V(ubuf[:, c, :], [("ubuf", c)])
                Vhu = V(hist_u[:, l, c, :], [("hist_u", l, c)])
                act(sgc, banks[bG][:, :], AF.Tanh, [V_bank(bG), Vp], [Vsg],
                    scale=0.5, bias=pcol(l, "hb_in", 8 + c))
                act(vabc, banks[bA][:, :], AF.Identity, [V_bank(bA), Vp], [Vvab], bias=pcol(l, "b_in", c))
                P.op("pool", lambda e, o=ubuf[:, c, 0:KA - 1], i=hist_u[:, l, c, :]: e.tensor_copy(out=o, in_=i),
                     ins=[Vhu], outs=[Vub])
                stt("dve", ubuf[:, c, KA - 1:], sgc, 1.0, vabc, ALU.add, ALU.mult,
                    [Vsg, Vvab], [Vub])
                P.op("pool", lambda e, o=hist_u[:, l, c, :], i=ubuf[:, c, T:T + KA - 1]: e.tensor_copy(out=o, in_=i),
                     ins=[Vub], outs=[Vhu])
                return wn

            conv_bank = {}

            def conv_a(c, wn):
                kp = KP if c < 4 else KA
                Vub = V(ubuf[:, c, :], [("ubuf", c)])
                Vc = V_cA(c)
                bC = next_bank()
                wn = pe_group(wn, [ubuf[:, c, k:k + T] for k in range(kp)], [Vub], bC)
                Vcb = V(cbq[:, c % 2, 0, :], [("cbq", c % 2, 0)])
                Vsq = V(cbq[:, c % 2, 1, :], [("cbq", c % 2, 1)])
                cb = pcol(l, "cab", c)
                if kp == KA:
                    src, Vsrc = banks[bC][:, :], V_bank(bC)
                else:
                    cw = PO["caw"] + c * KA
                    ac = arena[:, c, :]
                    ts("dve", ac, ubuf[:, c, kp:kp + T], prm[:, l, cw + kp:cw + kp + 1], None, ALU.mult, None,
                       [Vub, Vp], [Vc])
                    for k in range(kp + 1, KA):
                        stt("dve", ac, ubuf[:, c, k:k + T], prm[:, l, cw + k:cw + k + 1], ac, ALU.mult, ALU.add,
                            [Vub, Vp, Vc], [Vc])
                    tt("dve", ac, banks[bC][:, :], ac, ALU.add, [V_bank(bC), Vc], [Vc])
                    src, Vsrc = ac, Vc
                act(cbq[:, c % 2, 0, :], src, AF.Identity, [Vsrc, Vp], [Vcb], scale=0.5, bias=cb)
                act(cbq[:, c % 2, 1, :], src, AF.Square, [Vsrc, Vp], [Vsq], scale=0.5, bias=cb)
                act(arena[:, c, :], src, AF.Identity, [Vsrc, Vp], [Vc], scale=0.5, bias=cb)
                return wn

            def ln_stats(c):
                Vcb = V(cbq[:, c % 2, 0, :], [("cbq", c % 2, 0)])
                Vsq = V(cbq[:, c % 2, 1, :], [("cbq", c % 2, 1)])
                P.op("pe", lambda e, c=c: e.matmul(banks[0][:, :], ones[:, :], cbq[:, c % 2, 0, :],
                                                   start=(c == 0), stop=(c == NC8 - 1)),
                     ins=[V_ones, Vcb], outs=[V_bank(0)])
                P.op("pe", lambda e, c=c: e.matmul(banks[1][:, :], ones[:, :], cbq[:, c % 2, 1, :],
                                                   start=(c == 0), stop=(c == NC8 - 1)),
                     ins=[V_ones, Vsq], outs=[V_bank(1)])

            def ln_finish():
                act(st_a[:, :], banks[0][:, :], AF.Square, [V_bank(0)], [V_st_a])
                stt("dve", st_a[:, :], banks[1][:, :], EPS, st_a[:, :], ALU.add, ALU.subtract,
                    [V_bank(1), V_st_a], [V_st_a])
                rsqrt_into(st_rstd[:, :], V_rstd, st_a[:, :], V_st_a)
                tt("dve", st_b[:, :], banks[0][:, :], st_rstd[:, :], ALU.mult, [V_bank(0), V_rstd], [V_st_b])

            def ln_norm(c):
                Vc = V_cA(c)
                eng = "dve" if c % 2 == 0 else "pool"
                tt(eng, arena[:, c, :], arena[:, c, :], st_rstd[:, :], ALU.mult, [Vc, V_rstd], [Vc])
                tt(eng, arena[:, c, :], arena[:, c, :], st_b[:, :], ALU.subtract, [Vc, V_st_b], [Vc])
                act(V_ua(c).ap, arena[:, c, :], AF.Silu, [Vc, Vp], [V_ua(c)],
                    scale=pcol(l, "lng", c), bias=pcol(l, "lnb", c))

            for c in range(NC8):
                wn = proj_a(c, wn)
                if 1 <= c <= 4:
                    wn = conv_a(c - 1, wn)
                if 2 <= c <= 5:
                    ln_stats(c - 2)

            RRs = {}

            def b_front(g, wn):
                for j in range(3):
                    jj = 3 * g + j
                    b = next_bank()
                    wn = pe_group(wn, hk, [V_h_all], b)
                    Vxb = V(xbuf[:, j, :], [("xbuf", j)])
                    Vhx = V(hist_x[:, l, jj, :], [("hist_x", l, jj)])
                    Vv = V(vv[:, g % 2, j, :], [("vv", g % 2, j)])
                    Vvb = V(vb[:, j, :], [("vb", j)])
                    vvj = vv[:, g % 2, j, :]
                    P.op("pool", lambda e, o=xbuf[:, j, 0:KB - 1], i=hist_x[:, l, jj, :]: e.tensor_copy(out=o, in_=i),
                         ins=[Vhx], outs=[Vxb])
                    act(xbuf[:, j, KB - 1:], banks[b][:, :], AF.Identity, [V_bank(b), Vp], [Vxb],
                        bias=pcol(l, "b_in", 16 + jj))
                    P.op("pool", lambda e, o=hist_x[:, l, jj, :], i=xbuf[:, j, T:T + KB - 1]: e.tensor_copy(out=o, in_=i),
                         ins=[Vxb], outs=[Vhx])
                    cw = PO["cbw"] + jj * KB
                    ts("dve", vvj, xbuf[:, j, 0:T], prm[:, l, cw:cw + 1], pcol(l, "cbb", jj), ALU.mult, ALU.add,
                       [Vxb, Vp], [Vv])
                    for k in range(1, KB):
                        stt("dve", vvj, xbuf[:, j, k:k + T], prm[:, l, cw + k:cw + k + 1], vvj,
                            ALU.mult, ALU.add, [Vxb, Vp, Vv], [Vv])
                    P.op("pool", lambda e, o=vb[:, j, :], i=vvj: e.tensor_copy(out=o, in_=i), ins=[Vv], outs=[Vvb])
                if g == 2:
                    ln_finish()
                if g >= 2:
                    for c in range(4 * (g - 2), 4 * (g - 2) + 4):
                        ln_norm(c)
                return wn

            def b_mid(g, wn):
                RR = {}
                for j in range(3):
                    jj = 3 * g + j
                    deps = BD_DEPS[j]
                    Vvbs = [V(vb[:, ci, :], [("vb", ci)]) for ci in deps]
                    bR = next_bank()
                    wn = pe_group(wn, [vb[:, ci, :] for ci in deps], Vvbs, bR)
                    bI = next_bank()
                    wn = pe_group(wn, [vb[:, ci, :] for ci in deps], Vvbs, bI)
                    bGb = next_bank()
                    wn = pe_group(wn, hk, [V_h_all], bGb)
                    R = [rg[:, j, q, :] for q in range(4)]
                    VR = [V(R[q], [("rg", j, q)]) for q in range(4)]
                    RR[j] = (R, VR)
                    act(R[0], banks[bR][:, :], AF.Tanh, [V_bank(bR), Vp], [VR[0]], scale=0.5, bias=pcol(l, "hbra", jj))
                    act(R[1], banks[bI][:, :], AF.Tanh, [V_bank(bI), Vp], [VR[1]], scale=0.5, bias=pcol(l, "hbrx", jj))
                    act(R[2], R[0], AF.Exp, [VR[0], Vp], [VR[2]], scale=pcol(l, "hclam", jj), bias=pcol(l, "hclam", jj))
                    act(R[3], R[0], AF.Exp, [VR[0], Vp], [VR[3]], scale=pcol(l, "clam", jj), bias=pcol(l, "clam", jj))
                    ts("pool", R[3], R[3], -0.25, 0.25, ALU.mult, ALU.add, [VR[3]], [VR[3]])
                    G0 = ge[:, j, 0, :]
                    G1 = ge[:, j, 1, :]
                    VG0 = V(G0, [("ge", j, 0)])
                    VG1 = V(G1, [("ge", j, 1)])
                    hb = pcol(l, "hb_in", 28 + jj)
                    act(G0, banks[bGb][:, :], AF.Identity, [V_bank(bGb), Vp], [VG0], scale=0.5, bias=hb)
                    tt("pool", G1, G0, G0, ALU.mult, [VG0], [VG1])
                    ts("pool", G1, G1, 4.0 * GELU_C * GELU_K, GELU_K, ALU.mult, ALU.add, [VG1], [VG1])
                    tt("pool", G1, G1, G0, ALU.mult, [VG1, VG0], [VG1])
                    act(G1, G1, AF.Tanh, [VG1], [VG1])
                RRs[g] = RR
                return wn

            def b_back(g):
                RR = RRs[g]
                for j in range(3):
                    R, VR = RR[j]
                    act(R[3], R[3], AF.Sqrt, [VR[3]], [VR[3]])
                for j in range(3):
                    jj = 3 * g + j
                    R, VR = RR[j]
                    Vv = V(vv[:, g % 2, j, :], [("vv", g % 2, j)])
                    Vhs = V(hst[:, l, jj:jj + 1], [("hst", l, jj)])
                    G0 = ge[:, j, 0, :]
                    G1 = ge[:, j, 1, :]
                    VG0 = V(G0, [("ge", j, 0)])
                    VG1 = V(G1, [("ge", j, 1)])
                    if first:
                        P.op("pool", lambda e, o=rg[:, j, 3, 0:1]: e.memset(o, 0.5), ins=[VR[3]], outs=[VR[3]])
                    stt("dve", R[1], R[1], 1.0, vv[:, g % 2, j, :], ALU.add, ALU.mult, [VR[1], Vv], [VR[1]])
                    tt("dve", R[1], R[1], R[3], ALU.mult, [VR[1], VR[3]], [VR[1]])
                    P.op("dve", lambda e, o=R[0], a=R[2], b=R[1], h=hst[:, l, jj:jj + 1]:
                         e.tensor_tensor_scan(out=o, data0=a, data1=b, initial=h, op0=ALU.mult, op1=ALU.add),
                         ins=[VR[2], VR[1], Vhs], outs=[VR[0]])
                    P.op("pool", lambda e, o=hst[:, l, jj:jj + 1], i=rg[:, j, 0, T - 1:T]: e.tensor_copy(out=o, in_=i),
                         ins=[VR[0]], outs=[Vhs])
                    stt("dve", G0, G1, 1.0, G0, ALU.add, ALU.mult, [VG1, VG0], [VG0])
                    tt("dve", vg[:, jj, :], R[0], G0, ALU.mult, [VR[0], VG0], [V_vg(jj)])

            wn = b_front(0, wn)
            for g in range(4):
                wn = b_mid(g, wn)
                if g < 2:
                    wn = conv_a(4 + 2 * g, wn)
                    ln_stats(4 + 2 * g)
                    wn = conv_a(5 + 2 * g, wn)
                    ln_stats(5 + 2 * g)
                if g + 1 < 4:
                    wn = b_front(g + 1, wn)
                b_back(g)

            V_ua_all = V(None, [("ar", 8 + i) for i in range(4)])
            V_vg_all = V(None, [("vg", j) for j in range(NR)])
            def sasb(co):
                nonlocal wn
                p = co % 2
                bSa = next_bank()
                bSb = next_bank()
                wn = pe_group(wn, hk, [V_h_all], bSa)
                wn = pe_group(wn, hk, [V_h_all], bSb)
                VM0 = V(tmpA[:, 2 * p, :], [("tmpA", 2 * p)])
                VM1 = V(tmpA[:, 2 * p + 1, :], [("tmpA", 2 * p + 1)])
                act(tmpA[:, 2 * p, :], banks[bSa][:, :], AF.Tanh, [V_bank(bSa), Vp], [VM0], scale=0.5, bias=pcol(l, "hb_in", 40 + co))
                act(tmpA[:, 2 * p + 1, :], banks[bSb][:, :], AF.Tanh, [V_bank(bSb), Vp], [VM1], scale=0.5, bias=pcol(l, "hb_in", 48 + co))

            def yayb(co):
                nonlocal wn
                p = co % 2
                bYa = next_bank()
                bYb = next_bank()
                wn = pe_group(wn, [V_ua(k).ap for k in range(NC8)], [V_ua_all], bYa)
                wn = pe_group(wn, [vg[:, j, :] for j in range(NR)], [V_vg_all], bYb)
                M0 = tmpA[:, 2 * p, :]
                M1 = tmpA[:, 2 * p + 1, :]
                VM0 = V(M0, [("tmpA", 2 * p)])
                VM1 = V(M1, [("tmpA", 2 * p + 1)])
                stt("dve", M0, M0, 1.0, banks[bYa][:, :], ALU.add, ALU.mult, [V_bank(bYa), VM0], [VM0])
                stt("dve", M1, M1, 1.0, banks[bYb][:, :], ALU.add, ALU.mult, [V_bank(bYb), VM1], [VM1])
                tt("pool", V_m(co).ap, M0, M1, ALU.add, [VM0, VM1], [V_m(co)])

            sasb(0)
            sasb(1)
            for co in range(NC8):
                yayb(co)
                if co + 2 < NC8:
                    sasb(co + 2)

            V_m_all = V(None, [("ar", 12 + i) for i in range(4)])
            for co in range(NC8):
                b = next_bank()
                wn = pe_group(wn, [V_m(k).ap for k in range(NC8)], [V_m_all], b)
                stt("dve", xres[:, co, :], banks[b][:, :], 0.5, xres[:, co, :], ALU.mult, ALU.add,
                    [V_bank(b), V_x(co)], [V_x(co)])

            Vr = rms_stats()
            for c in range(NC8):
                stt("dve", hbf[:, c, :], xres[:, c, :], pcol(l, "gmlp", c), st_rstd[:, :], ALU.mult, ALU.mult,
                    [V_x(c), Vr, Vp], [V_h(c)])
            for fc in range(NF):
                b = next_bank()
                wn = pe_group(wn, hk, [V_h_all], b)
                if dve_relu2:
                    stt("dve", V_f(fc).ap, banks[b][:, :], 0.0, banks[b][:, :], ALU.max, ALU.mult,
                        [V_bank(b)], [V_f(fc)])
                else:
                    rlc = tmpA[:, fc % 4, :]
                    Vrl = V(rlc, [("tmpA", fc % 4)])
                    act(rlc, banks[b][:, :], AF.Relu, [V_bank(b)], [Vrl])
                    stt("dve", V_f(fc).ap, rlc, 0.0, banks[b][:, :], ALU.add, ALU.mult,
                        [Vrl, V_bank(b)], [V_f(fc)])
            V_f_all = V(None, [("ar", i) for i in range(16)])
            for co in range(NC8):
                b = next_bank()
                wn = pe_group(wn, [V_f(fc).ap for fc in range(NF)], [V_f_all], b)
                tt("dve", xres[:, co, :], banks[b][:, :], xres[:, co, :], ALU.add, [V_bank(b), V_x(co)], [V_x(co)])
            assert wn == N_WT, wn
            ws["gbase"] += NU

        cst = sb("cst", [128, 2], F32)
        V_cst = V(None, [("cst",)])
        P.op("pool", lambda e: e.memset(cst[:, 0:1], EPS), outs=[V_cst])
        P.op("pool", lambda e: e.memset(cst[:, 1:2], 1.0), outs=[V_cst])
        V_cpow = V(None, [("cpow",)])
        P.op("pool", lambda e: e.memset(cpow[:, 0, :], 0.5), outs=[V_cpow])
        P.op("pool", lambda e: e.memset(cpow[:, 1, :], -0.5), outs=[V_cpow])

        out_toks = []
        V_x_all = V(None, [("xres", c) for c in range(NC8)])
        V_out = V(None, [("ar", c) for c in range(NC8)])
        for s in range(NSEQ):
            P.op("pool", lambda e: e.memset(hist_u[:, :, :, :], 0.0),
                 outs=[V(None, [("hist_u", l, c) for l in range(L) for c in range(NC8)])])
            P.op("pool", lambda e: e.memset(hist_x[:, :, :, :], 0.0),
                 outs=[V(None, [("hist_x", l, j) for l in range(L) for j in range(NR)])])
            P.op("pool", lambda e: e.memset(hst[:, :, :], 0.0),
                 outs=[V(None, [("hst", l, j) for l in range(L) for j in range(NR)])])
            for t in range(NT):
                P.op("pool", lambda e, s=s, t=t: e.dma_start(out=xres[:, :, :], in_=x_d[s, t]),
                     outs=[V_x_all], dsem=s_x)
                for l in range(L):
                    layer(l, t == 0)
                Vr = rms_stats()
                for c in range(NC8):
                    stt("dve", arena[:, c, :], xres[:, c, :], pcol(0, "gfin", c), st_rstd[:, :], ALU.mult, ALU.mult,
                        [V_x(c), Vr, V_prm(0)], [V_cA(c)])
                tok = P.op("pool", lambda e, s=s, t=t: e.dma_start(out=o_d[s, t], in_=arena[:, 0:NC8, :]),
                           ins=[V_out], outs=[V(None, [("out", s, t)])], dsem=s_o)
                out_toks.append(tok)
        P.wait_all("sp", out_toks)
        P.wait_all("pool", out_toks)
        P.emit()
        build_nc.stats = dict(nops=dict(P.nops), nwaits=P.nwaits)
    return nc


N_CORES = 8
_CACHE = {}


def _prep(inputs, L):
    w = _pack_weights(L, w_in=inputs["w_in"], w_a_out=inputs["w_a_out"], w_b_out=inputs["w_b_out"],
                      w_o=inputs["w_o"], w_1=inputs["w_1"], w_2=inputs["w_2"],
                      w_rg_a=inputs["w_rg_a"], w_rg_x=inputs["w_rg_x"], conv_a_w=inputs["conv_a_w"])
    p = _pack_params(L, inputs["b_in"], inputs["conv_a_w"], inputs["conv_a_b"], inputs["ln_g"], inputs["ln_b"],
                     inputs["conv_b_w"], inputs["conv_b_b"], inputs["b_rg_a"], inputs["b_rg_x"], inputs["lam"],
                     inputs["g_mix"], inputs["g_mlp"], inputs["g_final"])
    return w, p


def _x_to_tiles(x):
    B, S, _ = x.shape
    return np.ascontiguousarray(x.reshape(B, S // T, T, NC8, 128).transpose(0, 1, 4, 3, 2))


def _tiles_to_x(o):
    B, NTt = o.shape[0], o.shape[1]
    return np.ascontiguousarray(o.transpose(0, 1, 4, 3, 2).reshape(B, NTt * T, D))


def run(inputs, L, n_cores, **bkw):
    inputs = {k: np.asarray(v, dtype=np.float32) for k, v in inputs.items()}
    x = inputs["x"]
    B, S, _ = x.shape
    nseq = B // n_cores
    NT = S // T
    w, p = _prep(inputs, L)
    xt = _x_to_tiles(x)
    key = (nseq, NT, L, tuple(sorted(bkw.items())))
    nc = build_nc(nseq, NT, L, **bkw)
    in_maps = [{"xT": xt[i * nseq:(i + 1) * nseq], "wts": w, "prm": p} for i in range(n_cores)]
    import os
    res = run_bass_kernel_spmd(nc, in_maps, core_ids=list(range(n_cores)), trace=bool(os.environ.get("K_TRACE")))
    run.last = res
    out = np.concatenate([r["outT"] for r in res.results], axis=0)
    return _tiles_to_x(out).astype(np.float32)


def kernel(**inputs):
    return run(inputs, L=4, n_cores=N_CORES)
```
